# Optimizing a Trainium2 kernel written in Bass

```python
import math
import jax, jax.numpy as jnp
from jax import lax
import numpy as np

D_MODEL = 2048
BATCH = 4
SEQ = 4096
DEPTH = 1

MIX_WIDTH = D_MODEL
BLOCK = 128
EPS = 1e-6

SWA_HEADS = 16
SWA_KV_HEADS = 2
SWA_HEAD_DIM = 64
SWA_GROUP = SWA_HEADS // SWA_KV_HEADS
WINDOW = 128

REL_BUCKETS = 32
REL_MAX_DIST = 128

MLA_HEADS = 8
MLA_Q_RANK = 384
MLA_KV_RANK = 128
MLA_NOPE_DIM = 128
MLA_ROPE_DIM = 64
MLA_V_DIM = 128
MLA_QK_DIM = MLA_NOPE_DIM + MLA_ROPE_DIM
ROPE_THETA = 10000.0

D_FF = 4 * D_MODEL

SWA_Q_COLS = SWA_HEADS * SWA_HEAD_DIM
SWA_KV_COLS = SWA_KV_HEADS * SWA_HEAD_DIM
OFF_SWA_Q = 0
OFF_SWA_K = OFF_SWA_Q + SWA_Q_COLS
OFF_SWA_V = OFF_SWA_K + SWA_KV_COLS
OFF_MLA_CQ = OFF_SWA_V + SWA_KV_COLS
OFF_MLA_CKV = OFF_MLA_CQ + MLA_Q_RANK
OFF_MLA_KR = OFF_MLA_CKV + MLA_KV_RANK
IN_COLS = OFF_MLA_KR + MLA_ROPE_DIM
SWA_OUT = SWA_HEADS * SWA_HEAD_DIM
MLA_OUT = MLA_HEADS * MLA_V_DIM

kernel_name = "hymba_swa_sink_mla_adaln_layer"


def rmsnorm(x, g):
    x32 = x.astype(jnp.float32)
    y = x32 * lax.rsqrt(jnp.mean(x32 * x32, axis=-1, keepdims=True) + EPS)
    return y.astype(x.dtype) * g


def t5_causal_bucket(dist):
    n = jnp.maximum(dist, 0)
    max_exact = REL_BUCKETS // 2
    is_small = n < max_exact
    nf = jnp.maximum(n, 1).astype(jnp.float32)
    large = max_exact + (jnp.log(nf / max_exact) / math.log(REL_MAX_DIST / max_exact)
                         * (REL_BUCKETS - max_exact)).astype(jnp.int32)
    large = jnp.minimum(large, REL_BUCKETS - 1)
    return jnp.where(is_small, n, large)


def rope(x, positions):
    half = x.shape[-1] // 2
    inv_freq = ROPE_THETA ** (-jnp.arange(half, dtype=jnp.float32) / half)
    ang = positions.astype(jnp.float32)[:, None] * inv_freq[None, :]
    extra = x.ndim - 3
    ang = ang.reshape((1, ang.shape[0]) + (1,) * extra + (half,))
    cos = jnp.cos(ang).astype(x.dtype)
    sin = jnp.sin(ang).astype(x.dtype)
    x1, x2 = x[..., :half], x[..., half:]
    return jnp.concatenate([x1 * cos - x2 * sin, x2 * cos + x1 * sin], axis=-1)


def swa_sink_attention(q, k, v, sinks, rel_bias):
    B, S = q.shape[0], q.shape[1]
    nb = S // BLOCK
    qb = q.reshape(B, nb, BLOCK, SWA_KV_HEADS, SWA_GROUP, SWA_HEAD_DIM)
    kb = k.reshape(B, nb, BLOCK, SWA_KV_HEADS, SWA_HEAD_DIM)
    vb = v.reshape(B, nb, BLOCK, SWA_KV_HEADS, SWA_HEAD_DIM)
    zero = jnp.zeros_like(kb[:, :1])
    k_band = jnp.concatenate([jnp.concatenate([zero, kb[:, :-1]], axis=1), kb], axis=2)
    v_band = jnp.concatenate([jnp.concatenate([zero, vb[:, :-1]], axis=1), vb], axis=2)
    s = jnp.einsum('bnqhgd,bnkhd->bnhgqk', qb, k_band).astype(jnp.float32) * (SWA_HEAD_DIM ** -0.5)

    q_loc = jnp.arange(BLOCK)[:, None]
    k_loc = jnp.arange(2 * BLOCK)[None, :]
    dist = q_loc + BLOCK - k_loc
    in_window = (dist >= 0) & (dist < WINDOW)
    blk = jnp.arange(nb)[:, None]
    key_valid = (blk * BLOCK - BLOCK + k_loc) >= 0
    mask = in_window[None] & key_valid[:, None, :]

    bias = rel_bias.astype(jnp.float32)[t5_causal_bucket(dist)]
    bias = bias.transpose(2, 0, 1).reshape(SWA_KV_HEADS, SWA_GROUP, BLOCK, 2 * BLOCK)
    s = s + bias[None, None]
    s = jnp.where(mask[None, :, None, None], s, -jnp.inf)

    sink = sinks.astype(jnp.float32).reshape(SWA_KV_HEADS, SWA_GROUP)[None, None, :, :, None, None]
    m = jnp.maximum(jnp.max(s, axis=-1, keepdims=True), sink)
    p = jnp.exp(s - m)
    p = p / (jnp.sum(p, axis=-1, keepdims=True) + jnp.exp(sink - m))
    o = jnp.einsum('bnhgqk,bnkhd->bnqhgd', p.astype(v.dtype), v_band)
    return o.reshape(B, S, SWA_OUT)


def mla_attention(q_nope, q_rope, k_nope, k_rope, v):
    B, S = q_nope.shape[0], q_nope.shape[1]
    nb = S // BLOCK
    qn = q_nope.reshape(B, nb, BLOCK, MLA_HEADS, MLA_NOPE_DIM).transpose(1, 0, 2, 3, 4)
    qr = q_rope.reshape(B, nb, BLOCK, MLA_HEADS, MLA_ROPE_DIM).transpose(1, 0, 2, 3, 4)
    key_pos = jnp.arange(S)
    scale = MLA_QK_DIM ** -0.5

    def one_block(args):
        i, qn_i, qr_i = args
        s = (jnp.einsum('bqhd,bkhd->bhqk', qn_i, k_nope)
             + jnp.einsum('bqhd,bkd->bhqk', qr_i, k_rope)).astype(jnp.float32) * scale
        q_pos = i * BLOCK + jnp.arange(BLOCK)
        causal = key_pos[None, :] <= q_pos[:, None]
        s = jnp.where(causal[None, None], s, -jnp.inf)
        p = jax.nn.softmax(s, axis=-1)
        return jnp.einsum('bhqk,bkhd->bqhd', p.astype(v.dtype), v)

    o = lax.map(one_block, (jnp.arange(nb), qn, qr))
    return o.transpose(1, 0, 2, 3, 4).reshape(B, S, MLA_OUT)


def setup_inputs(seed: int = 0) -> dict:
    key = jax.random.key(seed)
    ks = jax.random.split(key, 20)
    f32 = jnp.float32
    nrm = lambda k, shape, s: jax.random.normal(k, shape, f32) * s
    return {
        "x": nrm(ks[0], (BATCH, SEQ, D_MODEL), 1.0),
        "c": nrm(ks[1], (BATCH, D_MODEL), 1.0),
        "w_mod": nrm(ks[2], (DEPTH, D_MODEL, 6 * D_MODEL), 0.5 * D_MODEL ** -0.5),
        "b_mod": nrm(ks[3], (DEPTH, 6 * D_MODEL), 0.01),
        "attn_norm_g": 1.0 + nrm(ks[4], (DEPTH, D_MODEL), 0.02),
        "w_in": nrm(ks[5], (DEPTH, D_MODEL, IN_COLS), D_MODEL ** -0.5),
        "swa_sinks": nrm(ks[6], (DEPTH, SWA_HEADS), 1.0),
        "rel_bias": nrm(ks[7], (REL_BUCKETS, SWA_HEADS), 0.5),
        "mla_q_norm_g": 1.0 + nrm(ks[8], (DEPTH, MLA_Q_RANK), 0.02),
        "w_uq": nrm(ks[9], (DEPTH, MLA_Q_RANK, MLA_HEADS * MLA_QK_DIM), MLA_Q_RANK ** -0.5),
        "mla_kv_norm_g": 1.0 + nrm(ks[10], (DEPTH, MLA_KV_RANK), 0.02),
        "w_ukv": nrm(ks[11], (DEPTH, MLA_KV_RANK, MLA_HEADS * (MLA_NOPE_DIM + MLA_V_DIM)), MLA_KV_RANK ** -0.5),
        "w_out": nrm(ks[12], (DEPTH, MIX_WIDTH, D_MODEL), MIX_WIDTH ** -0.5),
        "mlp_norm_g": 1.0 + nrm(ks[13], (DEPTH, D_MODEL), 0.02),
        "w_ff1": nrm(ks[14], (DEPTH, D_MODEL, D_FF), D_MODEL ** -0.5),
        "w_ff2": nrm(ks[15], (DEPTH, D_FF, D_MODEL), D_FF ** -0.5),
        "final_norm_g": 1.0 + nrm(ks[16], (D_MODEL,), 0.02),
    }


def reference(x, c, w_mod, b_mod, attn_norm_g, w_in, swa_sinks, rel_bias, mla_q_norm_g, w_uq,
              mla_kv_norm_g, w_ukv, w_out, mlp_norm_g, w_ff1, w_ff2, final_norm_g):
    B, S, _ = x.shape
    positions = jnp.arange(S)
    c_act = jax.nn.silu(c)
    for l in range(DEPTH):
        mod = c_act @ w_mod[l] + b_mod[l]
        sh1, sc1, g1, sh2, sc2, g2 = [m[:, None, :] for m in jnp.split(mod, 6, axis=-1)]

        h = rmsnorm(x, attn_norm_g[l]) * (1.0 + sc1) + sh1
        proj = jnp.einsum('bsd,df->bsf', h, w_in[l])

        q_a = proj[..., OFF_SWA_Q:OFF_SWA_K].reshape(B, S, SWA_HEADS, SWA_HEAD_DIM)
        k_a = proj[..., OFF_SWA_K:OFF_SWA_V].reshape(B, S, SWA_KV_HEADS, SWA_HEAD_DIM)
        v_a = proj[..., OFF_SWA_V:OFF_MLA_CQ].reshape(B, S, SWA_KV_HEADS, SWA_HEAD_DIM)
        o_a = swa_sink_attention(q_a, k_a, v_a, swa_sinks[l], rel_bias)

        c_q = rmsnorm(proj[..., OFF_MLA_CQ:OFF_MLA_CKV], mla_q_norm_g[l])
        c_kv = rmsnorm(proj[..., OFF_MLA_CKV:OFF_MLA_KR], mla_kv_norm_g[l])
        k_rope = rope(proj[..., OFF_MLA_KR:IN_COLS], positions)
        q_b = jnp.einsum('bsr,rf->bsf', c_q, w_uq[l]).reshape(B, S, MLA_HEADS, MLA_QK_DIM)
        q_nope = q_b[..., :MLA_NOPE_DIM]
        q_rope = rope(q_b[..., MLA_NOPE_DIM:], positions)
        kv_b = jnp.einsum('bsr,rf->bsf', c_kv, w_ukv[l]).reshape(B, S, MLA_HEADS, MLA_NOPE_DIM + MLA_V_DIM)
        k_nope = kv_b[..., :MLA_NOPE_DIM]
        v_b = kv_b[..., MLA_NOPE_DIM:]
        o_b = mla_attention(q_nope, q_rope, k_nope, k_rope, v_b)

        mix = jnp.concatenate([o_a, o_b], axis=-1)
        x = x + g1 * jnp.einsum('bsm,md->bsd', mix, w_out[l])

        h = rmsnorm(x, mlp_norm_g[l]) * (1.0 + sc2) + sh2
        u = jax.nn.relu(jnp.einsum('bsd,df->bsf', h, w_ff1[l]))
        x = x + g2 * jnp.einsum('bsf,fd->bsd', u * u, w_ff2[l])
    return rmsnorm(x, final_norm_g)
```

```python
import numpy as np
import ml_dtypes
import concourse.bass as bass
import concourse.mybir as mybir
from concourse.bass_utils import run_bass_kernel_spmd

F32 = mybir.dt.float32
BF16 = mybir.dt.bfloat16
ALU = mybir.AluOpType
AF = mybir.ActivationFunctionType
AX = mybir.AxisListType

D = 2048
S = 4096
NEG = -30000.0
EPS = 1e-6
OWN = {0: [0, 3, 4, 7], 1: [1, 2, 5, 6]}
TMAX = [1, 3, 5, 7]
TMIN = [0, 2, 4, 6]
MLA_SCALE = 192.0 ** -0.5
STAGE = 99


class StopGen(Exception):
    pass


class Res:
    __slots__ = ("name", "w", "r")

    def __init__(self, name):
        self.name = name
        self.w = None
        self.r = {}


class Prog:
    ENG = ["pe", "act", "dve", "pool", "sp"]

    def __init__(self, dry=False):
        self.dry = dry
        self.ops = {e: [] for e in self.ENG}
        self.nds = 40
        self.nds_sp = 28
        self.ds_val = [0] * self.nds
        self.ds_last = [None] * self.nds
        self.ds_rr = {"sp": 0, "pool": 0}
        self.last_tok = {}

    def _deps(self, eng, reads, writes):
        waits = set()
        for r in reads:
            if r.w is not None:
                waits.add(r.w)
        for w in writes:
            if w.w is not None:
                waits.add(w.w)
            for t in w.r.values():
                waits.add(t)
        if eng == "pe":
            waits = {t for t in waits if not (t[0] == "E" and t[1] == "pe")}
        return waits

    def _commit(self, tok, key, reads, writes):
        for r in reads:
            r.r[key] = tok
        for w in writes:
            w.w = tok
            w.r = {}

    def op(self, eng, fn, reads=(), writes=()):
        if self.dry:
            return None
        idx = len(self.ops[eng])
        tok = ("E", eng, idx)
        waits = self._deps(eng, reads, writes)
        self.ops[eng].append([fn, waits, None, False])
        self._commit(tok, eng, reads, writes)
        self.last_tok[eng] = tok
        return tok

    def dma(self, eng, fn, reads=(), writes=()):
        if self.dry:
            return None
        if eng == "sp":
            si = self.ds_rr["sp"]
            self.ds_rr["sp"] = (si + 1) % self.nds_sp
        else:
            si = self.nds_sp + self.ds_rr["pool"]
            self.ds_rr["pool"] = (self.ds_rr["pool"] + 1) % (self.nds - self.nds_sp)
        waits = self._deps(eng, reads, writes)
        if self.ds_last[si] is not None:
            waits.add(self.ds_last[si])
        self.ds_val[si] += 16
        tok = ("D", si, self.ds_val[si])
        self.ds_last[si] = tok
        self.ops[eng].append([fn, waits, tok, False])
        self._commit(tok, ("D", si), reads, writes)
        self.last_tok[("D", si)] = tok
        return tok

    def barrier(self):
        if self.dry:
            return
        toks = set(self.last_tok.values())
        for e in ["pe", "act", "dve", "pool", "sp"]:
            w = {t for t in toks if not (t[0] == "E" and t[1] == e)}
            self.ops[e].append([None, w, None, False])

    def emit(self, nc):
        for e in self.ENG:
            for o in self.ops[e]:
                for t in o[1]:
                    if t[0] == "E":
                        self.ops[t[1]][t[2]][3] = True
        counts = {}
        for e in self.ENG:
            c = 0
            lst = []
            for o in self.ops[e]:
                if o[3] and o[0] is not None and o[2] is None:
                    c += 1
                lst.append(c)
            counts[e] = lst
        import contextlib
        with contextlib.ExitStack() as st:
            esem = {e: st.enter_context(nc.semaphore("s_" + e)) for e in ["pe", "act", "dve", "pool"]}
            dsem = [st.enter_context(nc.semaphore("d%d" % i)) for i in range(self.nds)]
            block = st.enter_context(nc.Block())

            def run(e, h):
                have = {}
                for o in self.ops[e]:
                    fn, waits, dtok, needed = o
                    wl = []
                    for t in waits:
                        if t[0] == "E":
                            wl.append((("E", t[1]), esem[t[1]], counts[t[1]][t[2]]))
                        else:
                            wl.append((("D", t[1]), dsem[t[1]], t[2]))
                    for key, sem, val in sorted(wl, key=lambda z: str(z[0])):
                        if have.get(key, 0) < val:
                            h.wait_ge(sem, val)
                            have[key] = val
                    if fn is None:
                        continue
                    ins = fn(h)
                    if dtok is not None:
                        ins.then_inc(dsem[dtok[1]], 16)
                    elif needed:
                        ins.then_inc(esem[e], 1)
                if e == "sp" or e == "pool":
                    for si in range(self.nds):
                        t = self.ds_last[si]
                        if t is not None and have.get(("D", si), 0) < t[2]:
                            h.wait_ge(dsem[si], t[2])

            @block.tensor
            def _(h):
                run("pe", h)

            @block.scalar
            def _(h):
                run("act", h)

            @block.vector
            def _(h):
                run("dve", h)

            @block.gpsimd
            def _(h):
                run("pool", h)

            @block.sync
            def _(h):
                run("sp", h)


def build_program(stop=None):
    nc = bass.Bass("TRN2", target_bir_lowering=False)
    dbgkind = "ExternalOutput" if stop else "Internal"

    def din(name, shape, dt=F32):
        return nc.dram_tensor(name, list(shape), dt, kind="ExternalInput").ap()

    xall = din("xall", [S, D])
    xq = din("xq", [4 * 640, D])
    ccol = din("ccol", [128, 16])
    wmod = din("wmod", [6 * 16 * 128, D])
    bmb = din("bmb", [128, 6 * D])
    gab = din("gab", [128, D])
    gmb = din("gmb", [128, D])
    gfbin = din("gfb", [128, D])
    winq = din("winq", [4 * 128, 16 * 512])
    wink = din("wink", [128, 16 * 384])
    wmla = din("wmla", [128, 8192])
    wukT = din("wukT", [128, 1024])
    woutg = din("woutg", [4 * 128, 16 * 512])
    w1g = din("w1g", [16 * 128, 16 * 512])
    w2g = din("w2g", [16 * 128, 8 * 1024])
    gqc = din("gqc", [128, 4])
    sinkb_in = din("sinkb", [128, 16])
    bmtab = din("bmtab", [8 * 128, 512])
    halom = din("halom", [4 * 128, 128])
    diag_in = din("diag", [128, 2048])
    msk_in = din("msk", [128, 16])
    cosk_in = din("cosk", [128, S])
    sink_in = din("sink", [128, S])
    cosq_in = din("cosq", [128, 4 * 512])
    sinq_in = din("sinq", [128, 4 * 512])
    ident_in = din("ident", [128, 128])
    out = nc.dram_tensor("out", [2048, D], F32, kind="ExternalOutput").ap()
    kscr = nc.dram_tensor("kscr", [128, 3 * S], BF16, kind=dbgkind).ap()
    dbg = nc.dram_tensor("dbg", [128, 16384], F32, kind=dbgkind).ap()
    dbgb = nc.dram_tensor("dbgb", [128, 32768], BF16, kind=dbgkind).ap()
    modscr = nc.dram_tensor("modscr", [7 * 128, D], F32, kind=dbgkind).ap()

    ARENA = 209920
    base0 = nc._sbuf_addr_for_side(None)
    base = (base0 + 31) // 32 * 32
    nc.alloc_sbuf_tensor("arena", [128, ARENA // 2], BF16)
    cur = [base]
    cnt = [0]

    def carve(nbytes):
        o = cur[0]
        cur[0] += (nbytes + 63) // 64 * 64
        assert cur[0] - base <= ARENA, (cur[0] - base, ARENA)
        return o

    def view(off, shape, dt):
        cnt[0] += 1
        return nc.alloc_sbuf_tensor_at("v%d" % cnt[0], list(shape), dt, offset=off)

    o_ring = [carve(16384) for _ in range(3)]
    o_rx = carve(32768)
    o_ry = carve(20480)
    o_rz = carve(32768)
    o_g = [carve(8192) for _ in range(2)]
    o_xs = [carve(8192) for _ in range(2)]
    o_xn = carve(4096)
    o_junk = carve(4096)
    o_ssb = carve(8192)
    o_p = carve(4096)
    o_pT = carve(4096)
    o_O = carve(2048)
    o_qaT = carve(2048)
    o_cs = carve(4096)
    o_small = carve(4096)

    ring_bf = [view(o, [128, 8192], BF16) for o in o_ring]
    xres = view(o_rx, [128, 4, D], F32)
    kside = view(o_rx, [128, 3, S], BF16)
    BMc = view(o_rx + 24576, [128, 2, 512], F32)
    DIAG = view(o_rx + 28672, [128, 4, 512], BF16)
    hT = view(o_ry, [128, 16, 640], BF16)
    uT = view(o_rz, [128, 32, 512], BF16)
    mixT = view(o_rz, [128, 16, 512], BF16)
    mlah = view(o_rz + 16384, [128, 8, 512], BF16)
    cqnT = view(o_rz + 24576, [128, 3, 512], BF16)
    kaT = view(o_rz + 27648, [128, 2, 640], BF16)
    va_tok = view(o_rz + 30208, [128, 5, 128], BF16)
    vaT = view(o_rz + 31488, [128, 640], BF16)
    cosk = view(o_rz, [128, S], F32)
    sink = view(o_rz + 16384, [128, S], F32)
    gbuf = [view(o, [128, D], F32) for o in o_g]
    xs = [view(o, [128, D], F32) for o in o_xs]
    cqraw = view(o_xs[1], [128, 3, 512], F32)
    xn = view(o_xn, [128, D], BF16)
    junk = view(o_junk, [128, D], BF16)
    ssb = view(o_ssb, [128, 4, 512], F32)
    pbuf = view(o_p, [128, 4, 512], BF16)
    pT = view(o_pT, [128, 16, 128], BF16)
    Obuf = view(o_O, [128, 4, 128], F32)
    qaT = view(o_qaT, [128, 2, 512], BF16)
    cosq = view(o_cs, [128, 512], F32)
    sinq = view(o_cs + 2048, [128, 512], F32)
    so = [o_small]

    def small(shape, dt):
        nb = int(np.prod(shape[1:])) * (4 if dt == F32 else 2)
        o = so[0]
        so[0] += (nb + 31) // 32 * 32
        assert so[0] - o_small <= 4096
        return view(o, shape, dt)

    ident = small([128, 128], BF16)
    identf = small([128, 128], F32)
    onesf = small([128, 128], F32)
    ccs = small([128, 16], F32)
    sinkb = small([128, 16], F32)
    gq = small([128, 4], F32)
    msk = small([128, 16], F32)
    halo = small([128, 128], F32)
    st_ss = small([128, 8], F32)
    st_m = small([128, 8], F32)
    st_l = small([128, 8], F32)
    st_a = small([128, 8], F32)
    st_x = small([128, 16], F32)
    crep = view(o_ssb, [128, 16, 128], BF16)
    osb = view(o_O, [128, 128], BF16)

    ps = nc.alloc_psum_tensor("ps", [128, 8, 512], F32)

    def bank(b):
        return ps[:, b, :]

    def bankbf(b):
        return ps[:, b, :].bitcast(BF16)

    def gen(P, ring_plan, holder):
        R = {}

        def res(name):
            if name not in R:
                R[name] = Res(name)
            return R[name]

        BK = [res("bank%d" % i) for i in range(8)]

        def dump_bf(ap2d, off):
            P.barrier()
            n = ap2d.shape[1]
            P.dma("sp", lambda e: e.dma_start(out=dbgb[:, off:off + n], in_=ap2d), [], [res("dbgb")])
            P.barrier()

        def chk(name):
            if stop == name:
                P.barrier()
                P.dma("sp", lambda e: e.dma_start(out=dbg[:, 0:2048], in_=ssb[:, :, :].rearrange("p a s -> p (a s)")), [], [res("dbg")])
                P.dma("sp", lambda e: e.dma_start(out=dbg[:, 2048:2064], in_=st_x[:, :]), [], [res("dbg")])
                P.dma("sp", lambda e: e.dma_start(out=dbg[:, 2064:2072], in_=st_m[:, :]), [], [res("dbg")])
                P.dma("sp", lambda e: e.dma_start(out=dbg[:, 2072:2080], in_=st_l[:, :]), [], [res("dbg")])
                P.dma("sp", lambda e: e.dma_start(out=dbg[:, 2080:2088], in_=st_a[:, :]), [], [res("dbg")])
                P.dma("sp", lambda e: e.dma_start(out=dbg[:, 4096:4608], in_=Obuf[:, :, :].rearrange("p a s -> p (a s)")), [], [res("dbg")])
                P.dma("sp", lambda e: e.dma_start(out=dbgb[:, 0:2048], in_=pbuf[:, :, :].rearrange("p a s -> p (a s)")), [], [res("dbgb")])
                P.dma("sp", lambda e: e.dma_start(out=dbgb[:, 2048:4096], in_=pT[:, :, :].rearrange("p a s -> p (a s)")), [], [res("dbgb")])
                P.dma("sp", lambda e: e.dma_start(out=dbgb[:, 4096:5120], in_=qaT[:, :, :].rearrange("p a s -> p (a s)")), [], [res("dbgb")])
                P.dma("sp", lambda e: e.dma_start(out=dbgb[:, 16384:24576], in_=mixT[:, :, :].rearrange("p a s -> p (a s)")), [], [res("dbgb")])
                P.dma("sp", lambda e: e.dma_start(out=dbgb[:, 8192:12288], in_=mlah[:, :, :].rearrange("p a s -> p (a s)")), [], [res("dbgb")])
                P.barrier()
                raise StopGen()

        def dump_f(ap2d, off):
            P.barrier()
            n = ap2d.shape[1]
            P.dma("sp", lambda e: e.dma_start(out=dbg[:, off:off + n], in_=ap2d), [], [res("dbg")])
            P.barrier()
        ritems = []
        holder.append(ritems)
        rstate = {"n": 0, "issued": 0}

        rstate["done"] = 0
        rstate["slot"] = -1

        def ring_issue_upto(k):
            while rstate["issued"] < len(ring_plan) and rstate["issued"] <= k:
                i = rstate["issued"]
                if i - 3 >= rstate["done"]:
                    break
                src, ncols, tag = ring_plan[i]
                if tag > rstate["slot"]:
                    break
                sl = i % 3
                dst = ring_bf[sl][:, 0:ncols]
                P.dma("pool", lambda e, dst=dst, src=src: e.dma_start(out=dst, in_=src), [], [res("ring%d" % sl)])
                rstate["issued"] += 1

        def ring_get(src, ncols):
            i = rstate["n"]
            rstate["n"] += 1
            ritems.append((src, ncols, rstate["slot"]))
            if not P.dry:
                ring_issue_upto(i)
                assert rstate["issued"] > i, (i, rstate)
            return i % 3, res("ring%d" % (i % 3))

        def ring_release():
            rstate["done"] += 1
            if not P.dry:
                ring_issue_upto(rstate["done"] + 2)

        def ring_prefetch():
            pass

        P.dma("sp", lambda e: e.dma_start(out=identf[:, :], in_=ident_in), [], [res("identf")])
        P.op("dve", lambda e: e.tensor_copy(out=ident[:, :], in_=identf[:, :]), [res("identf")], [res("ident")])
        P.op("pool", lambda e: e.memset(onesf[:, :], 1.0), [], [res("onesf")])
        P.dma("sp", lambda e: e.dma_start(out=ccs[:, :], in_=ccol), [], [res("ccs")])
        P.dma("sp", lambda e: e.dma_start(out=sinkb[:, :], in_=sinkb_in), [], [res("sinkb")])
        P.dma("sp", lambda e: e.dma_start(out=gq[:, :], in_=gqc), [], [res("gq")])
        P.dma("sp", lambda e: e.dma_start(out=msk[:, :], in_=msk_in), [], [res("msk")])

        P.op("act", lambda e: e.activation(out=ccs[:, :], in_=ccs[:, :], func=AF.Silu), [res("ccs")], [res("ccs")])
        for k in range(16):
            P.op("act", lambda e, k=k: e.activation(out=crep[:, k, :], in_=onesf[:, :], func=AF.Copy,
                                                    scale=ccs[:, k:k + 1]),
                 [res("ccs"), res("onesf")], [res("crep")])
        P.dma("sp", lambda e: e.dma_start(out=xs[0][:, :], in_=gab), [], [res("xs0")])
        P.dma("sp", lambda e: e.dma_start(out=xs[1][:, :], in_=gmb), [], [res("xs1")])
        vorder = [1, 0, 2, 4, 3, 5]
        for v in vorder:
            for k in range(16):
                sl, rr = ring_get(wmod[(v * 16 + k) * 128:(v * 16 + k + 1) * 128, :], 2048)
                for g in range(4):
                    P.op("pe", lambda e, g=g, k=k, sl=sl: e.matmul(bank(g), crep[:, k, :],
                                                                   ring_bf[sl][:, g * 512:(g + 1) * 512],
                                                                   start=(k == 0), stop=(k == 15)),
                         [rr, res("crep")], [BK[g]])
                ring_release()
            gi = 0 if v in (1, 4, 2, 5) else 1
            gt = gbuf[gi]
            P.dma("sp", lambda e, gt=gt, v=v: e.dma_start(out=gt[:, :], in_=bmb[:, v * D:(v + 1) * D]), [],
                  [res("gbuf%d" % gi)])
            for g in range(4):
                P.op("dve", lambda e, g=g, gt=gt: e.tensor_tensor(out=gt[:, g * 512:(g + 1) * 512], in0=bank(g),
                                                                  in1=gt[:, g * 512:(g + 1) * 512], op=ALU.add),
                     [BK[g], res("gbuf%d" % gi)], [BK[g], res("gbuf%d" % gi)])
            if v == 1 or v == 4:
                gam = xs[0] if v == 1 else xs[1]
                gr = res("xs0") if v == 1 else res("xs1")
                P.op("dve", lambda e, gt=gt, gam=gam: e.scalar_tensor_tensor(out=gt[:, :], in0=gt[:, :], scalar=1.0,
                                                                               in1=gam[:, :], op0=ALU.add,
                                                                               op1=ALU.mult),
                     [res("gbuf%d" % gi), gr], [res("gbuf%d" % gi)])
            row = {1: 0, 0: 1, 2: 2, 4: 3, 3: 4, 5: 5}[v]
            P.dma("sp", lambda e, gt=gt, row=row: e.dma_start(out=modscr[row * 128:(row + 1) * 128, :], in_=gt[:, :]),
                  [res("gbuf%d" % gi)], [res("modscr%d" % row)])

        P.barrier()
        if stop == "M":
            return ritems

        def load_mod(row, gi):
            P.dma("sp", lambda e: e.dma_start(out=gbuf[gi][:, :], in_=modscr[row * 128:(row + 1) * 128, :]),
                  [res("modscr%d" % row)], [res("gbuf%d" % gi)])

        def norm_block(src_ap_fn, src_res, xbuf_i, Ab, Bb, dst_col, from_sbuf=None):
            xr = res("xs%d" % xbuf_i)
            if from_sbuf is None:
                xt = xs[xbuf_i]
                P.dma("sp", lambda e: e.dma_start(out=xt[:, :], in_=src_ap_fn()), src_res, [xr])
                srcr = [xr]
                xin = xt[:, :]
            else:
                xin = from_sbuf
                srcr = src_res
                xt = xs[xbuf_i]
            ssr = res("st_ss")
            P.op("dve", lambda e: e.memset(st_ss[:, 0:1], 0.0), [], [ssr])
            P.op("act", lambda e: e.activation(out=junk[:, :], in_=xin, func=AF.Square, accum_out=st_ss[:, 0:1]),
                 srcr + [ssr], [res("junk"), ssr])
            P.op("dve", lambda e: e.tensor_scalar(out=st_ss[:, 1:2], in0=st_ss[:, 0:1], scalar1=1.0 / D, scalar2=EPS,
                                                  op0=ALU.mult, op1=ALU.add), [ssr], [ssr])
            P.op("pool", lambda e: e.tensor_tensor(out=st_ss[:, 2:3], in0=st_ss[:, 1:2], in1=mhalf[:, 0:1], op=ALU.pow), [ssr], [ssr])
            P.op("dve", lambda e: e.scalar_tensor_tensor(out=xt[:, :], in0=xin, scalar=st_ss[:, 2:3], in1=Ab[0][:, :],
                                                         op0=ALU.mult, op1=ALU.mult),
                 srcr + [ssr, Ab[1]], [xr])
            P.op("pool", lambda e: e.tensor_tensor(out=xn[:, :], in0=xt[:, :], in1=Bb[0][:, :], op=ALU.add),
                 [xr, Bb[1]], [res("xn")])
            for k in range(16):
                b = k // 8
                P.op("pe", lambda e, k=k, b=b: e.transpose(bankbf(b)[:, (k % 8) * 128:(k % 8 + 1) * 128],
                                                           xn[:, k * 128:(k + 1) * 128], ident[:, :]),
                     [res("xn"), res("ident")], [BK[b]])
            P.op("act", lambda e: e.activation(out=hT[:, 0:8, dst_col:dst_col + 128],
                                               in_=bankbf(0).rearrange("p (k t) -> p k t", k=8), func=AF.Copy),
                 [BK[0]], [BK[0], res("hT")])
            P.op("dve", lambda e: e.tensor_copy(out=hT[:, 8:16, dst_col:dst_col + 128],
                                                in_=bankbf(1).rearrange("p (k t) -> p k t", k=8)),
                 [BK[1]], [BK[1], res("hT")])

        load_mod(0, 0)
        load_mod(1, 1)
        A1 = (gbuf[0], res("gbuf0"))
        B1 = (gbuf[1], res("gbuf1"))
        P.dma("sp", lambda e: e.dma_start(out=cosk[:, :], in_=cosk_in), [], [res("cosk")])
        P.dma("sp", lambda e: e.dma_start(out=sink[:, :], in_=sink_in), [], [res("sink")])
        slk, rk = ring_get(wink, 16 * 384)
        wk = ring_bf[slk][:, 0:16 * 384].rearrange("p (k c) -> p k c", k=16)
        for t in range(8):
            for blk in range(4):
                r0 = t * 512 + blk * 128
                norm_block(lambda r0=r0: xall[r0:r0 + 128, :], [], blk % 2, A1, B1, blk * 128)
            for m in range(3):
                for k in range(16):
                    P.op("pe", lambda e, m=m, k=k: e.matmul(bank(2 + m), wk[:, k, m * 128:(m + 1) * 128],
                                                            hT[:, k, 0:512], start=(k == 0), stop=(k == 15)),
                         [rk, res("hT")], [BK[2 + m]])
            P.op("act", lambda e: e.activation(out=ssb[:, 0, :], in_=bank(2), func=AF.Square), [BK[2]],
                 [BK[2], res("ssb0")])
            P.op("pe", lambda e: e.matmul(bank(5), onesf[:, :], ssb[:, 0, :], start=True, stop=True),
                 [res("onesf"), res("ssb0")], [BK[5]])
            P.op("dve", lambda e: e.tensor_scalar(out=ssb[:, 1, :], in0=bank(5), scalar1=1.0 / 128, scalar2=EPS,
                                                  op0=ALU.mult, op1=ALU.add), [BK[5]], [BK[5], res("ssb1")])
            P.op("pool", lambda e: e.tensor_tensor(out=ssb[:, 1, :], in0=ssb[:, 1, :], in1=mhalf[:, :], op=ALU.pow), [res("ssb1")], [res("ssb1")])
            P.op("dve", lambda e, t=t: e.scalar_tensor_tensor(out=kside[:, 0, t * 512:(t + 1) * 512], in0=bank(2),
                                                              scalar=gq[:, 3:4], in1=ssb[:, 1, :], op0=ALU.mult,
                                                              op1=ALU.mult),
                 [BK[2], res("gq"), res("ssb1")], [BK[2], res("kside")])
            for c in range(4):
                P.op("pe", lambda e, c=c, t=t: e.transpose(bankbf(6)[:, c * 128:(c + 1) * 128],
                                                           kside[:, 0, t * 512 + c * 128:t * 512 + (c + 1) * 128],
                                                           ident[:, :]),
                     [res("kside"), res("ident")], [BK[6]])
            P.op("act", lambda e, t=t: e.activation(out=kside[:, 2, t * 512:(t + 1) * 512], in_=bankbf(6)[:, 0:512],
                                                    func=AF.Copy), [BK[6]], [BK[6], res("kside")])
            P.op("dve", lambda e, t=t: e.tensor_tensor(out=ssb[:, 2, :], in0=bank(3), in1=cosk[:, t * 512:(t + 1) * 512],
                                                       op=ALU.mult), [BK[3], res("cosk")], [BK[3], res("ssb2")])
            P.op("dve", lambda e, t=t: e.tensor_tensor(out=ssb[:, 3, :], in0=bank(4), in1=sink[:, t * 512:(t + 1) * 512],
                                                       op=ALU.mult), [BK[4], res("sink")], [BK[4], res("ssb3")])
            P.op("pool", lambda e, t=t: e.tensor_tensor(out=kside[:, 1, t * 512:(t + 1) * 512], in0=ssb[:, 2, :],
                                                        in1=ssb[:, 3, :], op=ALU.add),
                 [res("ssb2"), res("ssb3")], [res("kside")])
        ring_release()
        P.dma("sp", lambda e: e.dma_start(out=kscr, in_=kside[:, :, :].rearrange("p a s -> p (a s)")),
              [res("kside")], [res("kscr")])
        P.barrier()
        if stop == "K":
            return ritems

        for s in range(4):
            nkt = TMAX[s] + 1
            rstate["slot"] = s
            P.dma("sp", lambda e: e.dma_start(out=kside[:, :, :].rearrange("p a s -> p (a s)"), in_=kscr),
                  [res("kscr")], [res("kside")])
            load_mod(0, 0)
            load_mod(1, 1)
            P.dma("sp", lambda e, s=s: e.dma_start(out=cosq[:, :], in_=cosq_in[:, s * 512:(s + 1) * 512]), [],
                  [res("cosq")])
            P.dma("sp", lambda e, s=s: e.dma_start(out=sinq[:, :], in_=sinq_in[:, s * 512:(s + 1) * 512]), [],
                  [res("sinq")])
            P.dma("sp", lambda e, s=s: e.dma_start(out=halo[:, :], in_=halom[s * 128:(s + 1) * 128, :]), [],
                  [res("halo")])
            for blk in range(5):
                r0 = s * 640 + blk * 128
                norm_block(lambda r0=r0: xq[r0:r0 + 128, :], [], blk % 2, A1, B1, blk * 128)
            sl2, r2 = ring_get(winq[2 * 128:3 * 128, :], 8192)
            w2v = ring_bf[sl2].rearrange("p (k c) -> p k c", k=16)
            sl3, r3 = ring_get(winq[3 * 128:4 * 128, :], 8192)
            w3v = ring_bf[sl3].rearrange("p (k c) -> p k c", k=16)
            ring_prefetch()
            for ci in range(3):
                for k in range(16):
                    P.op("pe", lambda e, ci=ci, k=k, w2v=w2v: e.matmul(bank(2), w2v[:, k, ci * 128:(ci + 1) * 128],
                                                              hT[:, k, 128:640], start=(k == 0), stop=(k == 15)),
                         [r2, res("hT")], [BK[2]])
                for k in range(16):
                    P.op("pe", lambda e, ci=ci, k=k, w2v=w2v: e.matmul(bank(3)[:, 0:128], w2v[:, k, ci * 128:(ci + 1) * 128],
                                                              hT[:, k, 0:128], start=(k == 0), stop=(k == 15)),
                         [r2, res("hT")], [BK[3]])
                if ci < 2:
                    P.op("act", lambda e, ci=ci: e.activation(out=kaT[:, ci, 128:640], in_=bank(2), func=AF.Copy),
                         [BK[2]], [BK[2], res("kaT")])
                    P.op("dve", lambda e, ci=ci: e.tensor_copy(out=kaT[:, ci, 0:128], in_=bank(3)[:, 0:128]),
                         [BK[3]], [BK[3], res("kaT")])
                else:
                    P.op("act", lambda e: e.activation(out=vaT[:, 128:640], in_=bank(2), func=AF.Copy),
                         [BK[2]], [BK[2], res("vaT")])
                    P.op("dve", lambda e: e.tensor_copy(out=vaT[:, 0:128], in_=bank(3)[:, 0:128]),
                         [BK[3]], [BK[3], res("vaT")])
            for blk in range(5):
                P.op("pe", lambda e, blk=blk: e.transpose(bankbf(4)[:, blk * 128:(blk + 1) * 128],
                                                          vaT[:, blk * 128:(blk + 1) * 128], ident[:, :]),
                     [res("vaT"), res("ident")], [BK[4]])
            P.op("act", lambda e: e.activation(out=va_tok[:, :, :].rearrange("p a s -> p (a s)"),
                                               in_=bankbf(4)[:, 0:640], func=AF.Copy), [BK[4]],
                 [BK[4], res("va_tok")])
            for ci in range(3):
                wv, rr, c0 = (w2v, r2, 384) if ci == 0 else (w3v, r3, (ci - 1) * 128)
                bk = 5 + (ci % 2)
                for k in range(16):
                    P.op("pe", lambda e, wv=wv, c0=c0, k=k, bk=bk: e.matmul(bank(bk), wv[:, k, c0:c0 + 128],
                                                                            hT[:, k, 128:640], start=(k == 0),
                                                                            stop=(k == 15)),
                         [rr, res("hT")], [BK[bk]])
                P.op("act", lambda e, ci=ci, bk=bk: e.activation(out=ssb[:, ci, :], in_=bank(bk), func=AF.Square),
                     [BK[bk]], [BK[bk], res("ssb%d" % ci)])
                P.op("dve", lambda e, ci=ci, bk=bk: e.tensor_copy(out=cqraw[:, ci, :], in_=bank(bk)),
                     [BK[bk]], [BK[bk], res("xs1")])
            ring_release()
            ring_release()
            for ci in range(3):
                P.op("pe", lambda e, ci=ci: e.matmul(bank(7), onesf[:, :], ssb[:, ci, :], start=(ci == 0),
                                                     stop=(ci == 2)),
                     [res("onesf"), res("ssb%d" % ci)], [BK[7]])
            P.op("dve", lambda e: e.tensor_scalar(out=ssb[:, 3, :], in0=bank(7), scalar1=1.0 / 384, scalar2=EPS,
                                                  op0=ALU.mult, op1=ALU.add), [BK[7]], [BK[7], res("ssb3")])
            P.op("pool", lambda e: e.tensor_tensor(out=ssb[:, 3, :], in0=ssb[:, 3, :], in1=mhalf[:, :], op=ALU.pow), [res("ssb3")], [res("ssb3")])
            for ci in range(3):
                P.op("dve", lambda e, ci=ci: e.scalar_tensor_tensor(out=cqnT[:, ci, :], in0=cqraw[:, ci, :],
                                                                    scalar=gq[:, ci:ci + 1], in1=ssb[:, 3, :],
                                                                    op0=ALU.mult, op1=ALU.mult),
                     [res("xs1"), res("gq"), res("ssb3")], [res("cqnT")])

            if stop == "Q2@%d" % s or (s == 0 and stop == "Q2"):
                dump_bf(cqnT[:, :, :].rearrange("p a s -> p (a s)"), 0)
                dump_bf(kaT[:, :, :].rearrange("p a s -> p (a s)"), 2048)
                dump_bf(va_tok[:, :, :].rearrange("p a s -> p (a s)"), 4096)
                dump_bf(hT[:, :, :].rearrange("p a s -> p (a s)"), 8192)
                return ritems
            qg = {}
            for j in range(8):
                if j % 4 == 0:
                    gi_ = j // 4
                    slq, rq = ring_get(winq[gi_ * 128:(gi_ + 1) * 128, :], 8192)
                    qg["v"] = ring_bf[slq].rearrange("p (k c) -> p k c", k=16)
                    qg["r"] = rq
                    ring_prefetch()
                jb = j % 2
                kv = j // 4
                P.dma("sp", lambda e, j=j, jb=jb: e.dma_start(out=BMc[:, jb, :], in_=bmtab[j * 128:(j + 1) * 128, :]),
                      [], [res("BMc%d" % jb)])
                qgv = qg["v"]
                for k in range(16):
                    P.op("pe", lambda e, j=j, k=k, qgv=qgv: e.matmul(bank(0), qgv[:, k, (j % 4) * 128:(j % 4 + 1) * 128],
                                                                     hT[:, k, 128:640], start=(k == 0), stop=(k == 15)),
                         [qg["r"], res("hT")], [BK[0]])
                if j % 4 == 3:
                    ring_release()
                P.op("act", lambda e, jb=jb: e.activation(out=qaT[:, jb, :], in_=bank(0), func=AF.Copy, scale=0.125),
                     [BK[0]], [BK[0], res("qaT%d" % jb)])
                chk("SWA1")
                for qb in range(4):
                    sbs = (1, 2) if qb % 2 == 0 else (6, 7)
                    for hh in range(2):
                        P.op("pe", lambda e, hh=hh, qb=qb, jb=jb, kv=kv, sbs=sbs: e.matmul(
                            bank(sbs[hh])[:, 0:256],
                            qaT[hh * 64:(hh + 1) * 64, jb, qb * 128:(qb + 1) * 128],
                            kaT[hh * 64:(hh + 1) * 64, kv, qb * 128:qb * 128 + 256], start=True, stop=True),
                             [res("qaT%d" % jb), res("kaT")], [BK[sbs[hh]]])
                    chk("SWA2")
                    sv = ssb[:, 2 * (qb % 2):2 * (qb % 2) + 2, :].rearrange("p a s -> p (a s)")[:, 0:512]
                    sr = res("ssb%d" % (2 * (qb % 2)))
                    sr2 = res("ssb%d" % (2 * (qb % 2) + 1))
                    for hh in range(2):
                        P.op("dve", lambda e, sv=sv, sbs=sbs, jb=jb, hh=hh: e.tensor_tensor(
                            out=sv[:, hh * 256:(hh + 1) * 256], in0=bank(sbs[hh])[:, 0:256],
                            in1=BMc[:, jb, hh * 256:(hh + 1) * 256], op=ALU.add),
                             [BK[sbs[hh]], res("BMc%d" % jb)], [BK[sbs[hh]], sr, sr2])
                    sv3 = sv.rearrange("p (h k) -> p h k", h=2)
                    if qb == 0:
                        for hh in range(2):
                            P.op("dve", lambda e, sv3=sv3, hh=hh: e.tensor_tensor(out=sv3[:, hh, 0:128],
                                                                                  in0=sv3[:, hh, 0:128],
                                                                                  in1=halo[:, :], op=ALU.add),
                                 [sr, sr2, res("halo")], [sr, sr2])
                    chk("SWA3")
                    stx = res("st_x")
                    P.op("dve", lambda e, sv3=sv3: e.tensor_reduce(out=st_x[:, 0:2], in_=sv3, axis=AX.X, op=ALU.max),
                         [sr, sr2], [stx])
                    P.op("dve", lambda e, j=j: e.tensor_tensor(out=st_x[:, 2:4], in0=st_x[:, 0:2],
                                                               in1=sinkb[:, 2 * j:2 * j + 2], op=ALU.max),
                         [stx, res("sinkb")], [stx])
                    P.op("dve", lambda e: e.tensor_scalar(out=st_x[:, 4:6], in0=st_x[:, 2:4], scalar1=-1.0,
                                                          scalar2=None, op0=ALU.mult), [stx], [stx])
                    P.op("dve", lambda e, j=j: e.tensor_tensor(out=st_x[:, 6:8], in0=sinkb[:, 2 * j:2 * j + 2],
                                                               in1=st_x[:, 2:4], op=ALU.subtract),
                         [stx, res("sinkb")], [stx])
                    P.op("dve", lambda e: e.memset(st_x[:, 8:10], 0.0), [], [stx])
                    chk("SWA4")
                    for hh in range(2):
                        P.op("act", lambda e, hh=hh, sv3=sv3: e.activation(out=pbuf[:, hh, 0:256], in_=sv3[:, hh, :],
                                                                           func=AF.Exp, bias=st_x[:, 4 + hh:5 + hh],
                                                                           accum_out=st_x[:, 8 + hh:9 + hh]),
                             [sr, sr2, stx], [res("pbuf0"), res("pbuf1"), stx])
                    P.op("act", lambda e: e.activation(out=st_x[:, 10:12], in_=st_x[:, 6:8], func=AF.Exp),
                         [stx], [stx])
                    P.op("dve", lambda e: e.tensor_tensor(out=st_x[:, 12:14], in0=st_x[:, 8:10], in1=st_x[:, 10:12],
                                                          op=ALU.add), [stx], [stx])
                    P.op("dve", lambda e: e.reciprocal(out=st_x[:, 14:16], in_=st_x[:, 12:14]), [stx], [stx])
                    chk("SWA5")
                    for hh in range(2):
                        for c in range(2):
                            P.op("pe", lambda e, hh=hh, c=c: e.transpose(
                                bankbf(3)[:, (hh * 2 + c) * 128:(hh * 2 + c + 1) * 128],
                                pbuf[:, hh, c * 128:(c + 1) * 128], ident[:, :]),
                                 [res("pbuf0"), res("pbuf1"), res("ident")], [BK[3]])
                    P.op("act", lambda e: e.activation(out=pT[:, 0:4, :].rearrange("p a s -> p (a s)"),
                                                       in_=bankbf(3)[:, 0:512], func=AF.Copy),
                         [BK[3]], [BK[3], res("pT")])
                    chk("SWA6")
                    for hh in range(2):
                        for c in range(2):
                            P.op("pe", lambda e, hh=hh, c=c, qb=qb, kv=kv: e.matmul(
                                bank(4)[:, hh * 64:(hh + 1) * 64], pT[:, hh * 2 + c, :],
                                va_tok[:, qb + c, kv * 64:(kv + 1) * 64], start=(c == 0), stop=(c == 1)),
                                 [res("pT"), res("va_tok")], [BK[4]])
                    for hh in range(2):
                        P.op("dve", lambda e, hh=hh: e.tensor_scalar(out=osb[:, hh * 64:(hh + 1) * 64],
                                                                     in0=bank(4)[:, hh * 64:(hh + 1) * 64],
                                                                     scalar1=st_x[:, 14 + hh:15 + hh], scalar2=None,
                                                                     op0=ALU.mult),
                             [BK[4], stx], [BK[4], res("Obuf")])
                    chk("SWA7")
                    P.op("pe", lambda e: e.transpose(bankbf(5)[:, 0:128], osb[:, :], ident[:, :]),
                         [res("Obuf"), res("ident")], [BK[5]])
                    P.op("act", lambda e, j=j, qb=qb: e.activation(out=mixT[:, j, qb * 128:(qb + 1) * 128],
                                                                   in_=bankbf(5)[:, 0:128], func=AF.Copy),
                         [BK[5]], [BK[5], res("mixT")])
                    chk("SWA8")
                chk("SWAj%d" % j)

            if stop == "SWA@%d" % s or (s == 0 and stop == "SWA"):
                dump_bf(mixT[:, :, :].rearrange("p a s -> p (a s)"), 0)
                return ritems
            P.dma("pool", lambda e: e.dma_start(out=DIAG[:, :, :].rearrange("p a s -> p (a s)"), in_=diag_in), [],
                  [res("DIAG")])
            slm, rm = ring_get(wmla, 8192)
            ring_prefetch()
            wuq = ring_bf[slm][:, 0:6144].rearrange("p (k c) -> p k c", k=3)
            wukv = ring_bf[slm][:, 6144:8192]
            for h in range(8):
                hb = h % 2
                qn = mlah[:, 0 + hb, :]
                qa = mlah[:, 4 + hb, :]
                ol = mlah[:, 6 + hb, :]
                for kk in range(3):
                    P.op("pe", lambda e, kk=kk, h=h, wuq=wuq: e.matmul(bank(7), wuq[:, kk, h * 128:(h + 1) * 128],
                                                              cqnT[:, kk, :], start=(kk == 0), stop=(kk == 2)),
                         [rm, res("cqnT")], [BK[7]])
                P.op("act", lambda e, qn=qn: e.activation(out=qn, in_=bank(7), func=AF.Copy), [BK[7]],
                     [BK[7], res("qn%d" % hb)])
                if h % 2 == 0:
                    i = h // 2
                    ib = i % 2
                    qr = mlah[:, 2 + ib, :]
                    for kk in range(3):
                        P.op("pe", lambda e, kk=kk, i=i, wuq=wuq: e.matmul(bank(4), wuq[:, kk, 1024 + i * 128:1024 + (i + 1) * 128],
                                                                  cqnT[:, kk, :], start=(kk == 0), stop=(kk == 2)),
                             [rm, res("cqnT")], [BK[4]])
                    for kk in range(3):
                        P.op("pe", lambda e, kk=kk, i=i, wuq=wuq: e.matmul(bank(5), wuq[:, kk, 1536 + i * 128:1536 + (i + 1) * 128],
                                                                  cqnT[:, kk, :], start=(kk == 0), stop=(kk == 2)),
                             [rm, res("cqnT")], [BK[5]])
                    P.op("dve", lambda e: e.tensor_tensor(out=ssb[:, 0, :], in0=bank(4), in1=cosq[:, :], op=ALU.mult),
                         [BK[4], res("cosq")], [BK[4], res("ssb0")])
                    P.op("dve", lambda e: e.tensor_tensor(out=ssb[:, 1, :], in0=bank(5), in1=sinq[:, :], op=ALU.mult),
                         [BK[5], res("sinq")], [BK[5], res("ssb1")])
                    P.op("pool", lambda e, qr=qr: e.tensor_tensor(out=qr, in0=ssb[:, 0, :], in1=ssb[:, 1, :], op=ALU.add),
                         [res("ssb0"), res("ssb1")], [res("qr%d" % ib)])
                i = h // 2
                ib = i % 2
                qr = mlah[:, 2 + ib, :]
                P.op("pe", lambda e, h=h, qn=qn: e.matmul(bank(7), wukT_bf[:, h * 128:(h + 1) * 128], qn, start=True,
                                                          stop=True), [res("wukT"), res("qn%d" % hb)], [BK[7]])
                P.op("act", lambda e, qa=qa: e.activation(out=qa, in_=bank(7), func=AF.Copy), [BK[7]],
                     [BK[7], res("qa%d" % hb)])
                stm, stl, sta = res("st_m"), res("st_l"), res("st_a")
                P.op("dve", lambda e: e.memset(st_m[:, 0:4], -1.0e30), [], [stm])
                P.op("dve", lambda e: e.memset(st_l[:, 0:4], 0.0), [], [stl])
                P.op("pool", lambda e: e.memset(Obuf[:, :, :], 0.0), [], [res("Obuf")])
                rb = (h % 2) * 64
                for kt in range(nkt):
                    special = 0 if kt == TMIN[s] else (1 if kt == TMAX[s] else -1)
                    for qb in range(4):
                        P.op("pe", lambda e, qb=qb, kt=kt, qa=qa: e.matmul(bank(qb), qa[:, qb * 128:(qb + 1) * 128],
                                                                           kside[:, 0, kt * 512:(kt + 1) * 512],
                                                                           start=True, stop=False),
                             [res("qa%d" % hb), res("kside")], [BK[qb]])
                        P.op("pe", lambda e, qb=qb, kt=kt, qr=qr, rb=rb: e.matmul(
                            bank(qb), qr[rb:rb + 64, qb * 128:(qb + 1) * 128],
                            kside[rb:rb + 64, 1, kt * 512:(kt + 1) * 512], start=False, stop=True),
                             [res("qr%d" % ib), res("kside")], [BK[qb]])
                    srcs = []
                    for qb in range(4):
                        if special >= 0:
                            a_col = msk[:, s * 4 + special * 2:s * 4 + special * 2 + 1]
                            P.op("dve", lambda e, qb=qb, a_col=a_col: e.scalar_tensor_tensor(
                                out=ssb[:, qb, :], in0=DIAG[:, qb, :], scalar=a_col, in1=bank(qb), op0=ALU.mult,
                                op1=ALU.add), [res("DIAG"), res("msk"), BK[qb]], [BK[qb], res("ssb%d" % qb)])
                            srcs.append((ssb[:, qb, :], res("ssb%d" % qb)))
                        else:
                            srcs.append((bank(qb), BK[qb]))
                    for qb in range(4):
                        P.op("dve", lambda e, qb=qb, a=srcs[qb][0]: e.tensor_reduce(out=st_m[:, 4 + qb:5 + qb], in_=a,
                                                                                    axis=AX.X, op=ALU.max),
                             [srcs[qb][1]], [srcs[qb][1], stm])
                    if special >= 0:
                        b_col = msk[:, s * 4 + special * 2 + 1:s * 4 + special * 2 + 2]
                        P.op("dve", lambda e, b_col=b_col: e.tensor_scalar(out=st_m[:, 4:8], in0=st_m[:, 4:8],
                                                                           scalar1=b_col, scalar2=None, op0=ALU.add),
                             [stm, res("msk")], [stm])
                    P.op("dve", lambda e: e.tensor_tensor(out=st_m[:, 4:8], in0=st_m[:, 4:8], in1=st_m[:, 0:4],
                                                          op=ALU.max), [stm], [stm])
                    P.op("dve", lambda e: e.tensor_tensor(out=st_a[:, 0:4], in0=st_m[:, 0:4], in1=st_m[:, 4:8],
                                                          op=ALU.subtract), [stm], [sta])
                    P.op("dve", lambda e: e.tensor_scalar(out=st_a[:, 4:8], in0=st_m[:, 4:8], scalar1=-MLA_SCALE,
                                                          scalar2=None, op0=ALU.mult), [stm, sta], [sta])
                    P.op("dve", lambda e: e.tensor_copy(out=st_m[:, 0:4], in_=st_m[:, 4:8]), [stm], [stm])
                    if special >= 0:
                        P.op("dve", lambda e, b_col=b_col: e.tensor_scalar(out=st_a[:, 4:8], in0=st_a[:, 4:8],
                                                                           scalar1=b_col, scalar2=None, op0=ALU.add),
                             [sta, res("msk")], [sta])
                    P.op("act", lambda e: e.activation(out=st_a[:, 0:4], in_=st_a[:, 0:4], func=AF.Exp,
                                                       scale=MLA_SCALE), [sta], [sta])
                    P.op("dve", lambda e: e.memset(st_l[:, 4:8], 0.0), [], [stl])
                    for qb in range(4):
                        P.op("act", lambda e, qb=qb, a=srcs[qb][0]: e.activation(out=pbuf[:, qb, :], in_=a, func=AF.Exp,
                                                                                 bias=st_a[:, 4 + qb:5 + qb],
                                                                                 scale=MLA_SCALE,
                                                                                 accum_out=st_l[:, 4 + qb:5 + qb]),
                             [srcs[qb][1], sta, stl], [srcs[qb][1], res("pbuf%d" % qb), stl])
                    P.op("dve", lambda e: e.tensor_tensor(out=st_l[:, 0:4], in0=st_l[:, 0:4], in1=st_a[:, 0:4],
                                                          op=ALU.mult), [stl, sta], [stl])
                    P.op("dve", lambda e: e.tensor_tensor(out=st_l[:, 0:4], in0=st_l[:, 0:4], in1=st_l[:, 4:8],
                                                          op=ALU.add), [stl], [stl])
                    for qb in range(4):
                        b_ = 4 + qb // 2
                        for c in range(4):
                            P.op("pe", lambda e, qb=qb, c=c, b_=b_: e.transpose(
                                bankbf(b_)[:, ((qb % 2) * 4 + c) * 128:((qb % 2) * 4 + c + 1) * 128],
                                pbuf[:, qb, c * 128:(c + 1) * 128], ident[:, :]),
                                 [res("pbuf%d" % qb), res("ident")], [BK[b_]])
                    P.op("act", lambda e: e.activation(out=pT[:, 0:8, :].rearrange("p a s -> p (a s)"),
                                                       in_=bankbf(4), func=AF.Copy), [BK[4]], [BK[4], res("pT")])
                    P.op("dve", lambda e: e.tensor_copy(out=pT[:, 8:16, :].rearrange("p a s -> p (a s)"),
                                                        in_=bankbf(5)), [BK[5]], [BK[5], res("pT")])
                    for qb in range(4):
                        for c in range(4):
                            P.op("pe", lambda e, qb=qb, c=c, kt=kt: e.matmul(
                                bank(6)[:, qb * 128:(qb + 1) * 128], pT[:, qb * 4 + c, :],
                                kside[:, 2, (kt * 4 + c) * 128:(kt * 4 + c + 1) * 128], start=(c == 0), stop=(c == 3)),
                                 [res("pT"), res("kside")], [BK[6]])
                    for qb in range(4):
                        P.op("dve", lambda e, qb=qb: e.scalar_tensor_tensor(
                            out=Obuf[:, qb, :], in0=Obuf[:, qb, :], scalar=st_a[:, qb:qb + 1],
                            in1=bank(6)[:, qb * 128:(qb + 1) * 128], op0=ALU.mult, op1=ALU.add),
                             [res("Obuf"), sta, BK[6]], [BK[6], res("Obuf")])
                P.op("dve", lambda e: e.reciprocal(out=st_l[:, 4:8], in_=st_l[:, 0:4]), [stl], [stl])
                for qb in range(4):
                    P.op("dve", lambda e, qb=qb: e.tensor_scalar(out=pbuf[:, qb, 0:128], in0=Obuf[:, qb, :],
                                                                 scalar1=st_l[:, 4 + qb:5 + qb], scalar2=None,
                                                                 op0=ALU.mult),
                         [res("Obuf"), stl], [res("pbuf%d" % qb)])
                    P.op("pe", lambda e, qb=qb: e.transpose(bankbf(7)[:, qb * 128:(qb + 1) * 128], pbuf[:, qb, 0:128],
                                                            ident[:, :]), [res("pbuf%d" % qb), res("ident")], [BK[7]])
                P.op("act", lambda e, ol=ol: e.activation(out=ol, in_=bankbf(7)[:, 0:512], func=AF.Copy), [BK[7]],
                     [BK[7], res("ol%d" % hb)])
                P.op("pe", lambda e, h=h, ol=ol, wukv=wukv: e.matmul(bank(7), wukv[:, h * 256 + 128:h * 256 + 256], ol, start=True,
                                                          stop=True), [rm, res("ol%d" % hb)], [BK[7]])
                P.op("act", lambda e, h=h: e.activation(out=mixT[:, 8 + h, :], in_=bank(7), func=AF.Copy), [BK[7]],
                     [BK[7], res("mixT")])

            ring_release()
            if stop == "MLA@%d" % s or (s == 0 and stop == "MLA"):
                dump_bf(mixT[:, :, :].rearrange("p a s -> p (a s)"), 0)
                return ritems
            P.barrier()
            load_mod(2, 0)
            G1 = (gbuf[0], res("gbuf0"))
            for tb in range(4):
                r0 = s * 640 + 128 + tb * 128
                P.dma("sp", lambda e, tb=tb, r0=r0: e.dma_start(out=xres[:, tb, :], in_=xq[r0:r0 + 128, :]), [],
                      [res("xres%d" % tb)])
            ei = 0
            for dg in range(4):
                slo, ro = ring_get(woutg[dg * 128:(dg + 1) * 128, :], 8192)
                wo = ring_bf[slo].rearrange("p (k c) -> p k c", k=16)
                ring_prefetch()
                for tb in range(4):
                    bk = (dg * 4 + tb) % 4
                    for c in range(16):
                        P.op("pe", lambda e, c=c, tb=tb, bk=bk, wo=wo: e.matmul(bank(bk), mixT[:, c, tb * 128:(tb + 1) * 128],
                                                                                wo[:, c, :], start=(c == 0),
                                                                                stop=(c == 15)),
                             [ro, res("mixT")], [BK[bk]])
                    tmp = ssb[:, bk, :]
                    P.op("dve", lambda e, bk=bk, dg=dg, tmp=tmp: e.tensor_tensor(out=tmp, in0=bank(bk),
                                                                                 in1=gbuf[0][:, dg * 512:(dg + 1) * 512],
                                                                                 op=ALU.mult),
                         [BK[bk], G1[1]], [BK[bk], res("ssb%d" % bk)])
                    P.op("pool", lambda e, tb=tb, dg=dg, tmp=tmp: e.tensor_tensor(
                        out=xres[:, tb, dg * 512:(dg + 1) * 512], in0=xres[:, tb, dg * 512:(dg + 1) * 512], in1=tmp,
                        op=ALU.add), [res("ssb%d" % bk), res("xres%d" % tb)], [res("xres%d" % tb)])
                ring_release()
            P.barrier()
            if stop == "Q8@%d" % s or (s == 0 and stop == "Q8"):
                dump_f(xres[:, :, :].rearrange("p a s -> p (a s)"), 0)
                return ritems
            load_mod(3, 0)
            load_mod(4, 1)
            A2 = (gbuf[0], res("gbuf0"))
            B2 = (gbuf[1], res("gbuf1"))
            for tb in range(4):
                norm_block(None, [res("xres%d" % tb)], tb % 2, A2, B2, tb * 128, from_sbuf=xres[:, tb, :])
            first_g2 = True
            for fh in range(2):
                for g in range(8):
                    gg = fh * 8 + g
                    sl1, r1 = ring_get(w1g[gg * 128:(gg + 1) * 128, :], 8192)
                    w1v = ring_bf[sl1].rearrange("p (k c) -> p k c", k=16)
                    ring_prefetch()
                    for jj in range(4):
                        jl = g * 4 + jj
                        bk = jl % 2
                        for k in range(16):
                            P.op("pe", lambda e, k=k, jj=jj, bk=bk, w1v=w1v: e.matmul(bank(bk), w1v[:, k, jj * 128:(jj + 1) * 128],
                                                                                      hT[:, k, 0:512], start=(k == 0),
                                                                                      stop=(k == 15)),
                                 [r1, res("hT")], [BK[bk]])
                        tmp = ssb[:, jl % 4, :]
                        P.op("act", lambda e, bk=bk, tmp=tmp: e.activation(out=tmp, in_=bank(bk), func=AF.Relu),
                             [BK[bk]], [BK[bk], res("ssb%d" % (jl % 4))])
                        P.op("pool", lambda e, jl=jl, tmp=tmp: e.tensor_tensor(out=uT[:, jl, :], in0=tmp, in1=tmp,
                                                                               op=ALU.mult),
                             [res("ssb%d" % (jl % 4))], [res("uT")])
                    ring_release()
                if first_g2:
                    load_mod(5, 0)
                    first_g2 = False
                G2 = (gbuf[0], res("gbuf0"))
                for dh in range(2):
                    for jg in range(4):
                        idx = (fh * 2 + dh) * 4 + jg
                        sl2_, r2_ = ring_get(w2g[idx * 128:(idx + 1) * 128, :], 8192)
                        w2v_ = ring_bf[sl2_].rearrange("p (j c) -> p j c", j=8)
                        ring_prefetch()
                        for jj in range(8):
                            jl = jg * 8 + jj
                            for tb in range(4):
                                for dgi in range(2):
                                    bk = tb * 2 + dgi
                                    P.op("pe", lambda e, jl=jl, jj=jj, tb=tb, dgi=dgi, bk=bk, w2v_=w2v_: e.matmul(
                                        bank(bk), uT[:, jl, tb * 128:(tb + 1) * 128],
                                        w2v_[:, jj, dgi * 512:(dgi + 1) * 512], start=(jl == 0), stop=(jl == 31)),
                                         [r2_, res("uT")], [BK[bk]])
                        ring_release()
                    for tb in range(4):
                        for dgi in range(2):
                            bk = tb * 2 + dgi
                            c0 = dh * 1024 + dgi * 512
                            tmp = ssb[:, bk % 4, :]
                            P.op("dve", lambda e, bk=bk, c0=c0, tmp=tmp: e.tensor_tensor(out=tmp, in0=bank(bk),
                                                                                         in1=gbuf[0][:, c0:c0 + 512],
                                                                                         op=ALU.mult),
                                 [BK[bk], G2[1]], [BK[bk], res("ssb%d" % (bk % 4))])
                            P.op("pool", lambda e, tb=tb, c0=c0, tmp=tmp: e.tensor_tensor(
                                out=xres[:, tb, c0:c0 + 512], in0=xres[:, tb, c0:c0 + 512], in1=tmp, op=ALU.add),
                                 [res("ssb%d" % (bk % 4)), res("xres%d" % tb)], [res("xres%d" % tb)])
            if stop == "FFN@%d" % s or (s == 0 and stop == "FFN"):
                dump_f(xres[:, :, :].rearrange("p a s -> p (a s)"), 0)
                return ritems
            if stop == "FFNDUMP@%d" % s:
                dump_f(xres[:, :, :].rearrange("p a s -> p (a s)"), 0)
            P.dma("sp", lambda e: e.dma_start(out=gbuf[1][:, :], in_=gfbin), [], [res("gbuf1")])
            for tb in range(4):
                ssr = res("st_ss")
                xr = res("xres%d" % tb)
                P.op("dve", lambda e: e.memset(st_ss[:, 0:1], 0.0), [], [ssr])
                P.op("act", lambda e, tb=tb: e.activation(out=junk[:, :], in_=xres[:, tb, :], func=AF.Square,
                                                          accum_out=st_ss[:, 0:1]), [xr, ssr], [res("junk"), ssr])
                P.op("dve", lambda e: e.tensor_scalar(out=st_ss[:, 1:2], in0=st_ss[:, 0:1], scalar1=1.0 / D, scalar2=EPS,
                                                      op0=ALU.mult, op1=ALU.add), [ssr], [ssr])
                P.op("pool", lambda e: e.tensor_tensor(out=st_ss[:, 2:3], in0=st_ss[:, 1:2], in1=mhalf[:, 0:1], op=ALU.pow), [ssr], [ssr])
                P.op("dve", lambda e, tb=tb: e.scalar_tensor_tensor(out=xres[:, tb, :], in0=xres[:, tb, :],
                                                                    scalar=st_ss[:, 2:3], in1=gbuf[1][:, :],
                                                                    op0=ALU.mult, op1=ALU.mult),
                     [xr, ssr, res("gbuf1")], [xr])
                r0 = s * 512 + tb * 128
                P.dma("sp", lambda e, tb=tb, r0=r0: e.dma_start(out=out[r0:r0 + 128, :], in_=xres[:, tb, :]), [xr],
                      [res("out")])
            P.barrier()
        return ritems

    o_mh = carve(2048)
    mhalf = view(o_mh, [128, 512], F32)
    o_wuk = carve(2048)
    wukT_bf = view(o_wuk, [128, 1024], BF16)

    def gen_safe(Pq, plan_):
        holder = []
        try:
            return gen(Pq, plan_, holder)
        except StopGen:
            return holder[0]

    dry = Prog(dry=True)
    plan = gen_safe(dry, [])
    P = Prog()
    Rw = {}
    P.dma("pool", lambda e: e.dma_start(out=wukT_bf[:, :], in_=wukT), [], [])
    P.op("pool", lambda e: e.memset(mhalf[:, :], -0.5), [], [])
    P.barrier()
    gen_safe(P, plan)
    P.emit(nc)
    return nc


def _t5_bucket(dist):
    n = np.maximum(dist, 0)
    max_exact = 16
    is_small = n < max_exact
    nf = np.maximum(n, 1).astype(np.float32)
    large = max_exact + (np.log(nf / max_exact) / np.log(128 / max_exact) * (32 - max_exact)).astype(np.int32)
    large = np.minimum(large, 31)
    return np.where(is_small, n, large)


_NC_CACHE = {}


def _prep(x, c, w_mod, b_mod, attn_norm_g, w_in, swa_sinks, rel_bias, mla_q_norm_g, w_uq, mla_kv_norm_g, w_ukv,
          w_out, mlp_norm_g, w_ff1, w_ff2, final_norm_g, cores=range(8)):
    f = np.float32
    x = np.asarray(x, f)
    c = np.asarray(c, f)
    w_mod = np.asarray(w_mod, f)[0]
    b_mod = np.asarray(b_mod, f)[0]
    w_in = np.asarray(w_in, f)[0]
    w_uq = np.asarray(w_uq, f)[0]
    w_ukv = np.asarray(w_ukv, f)[0]
    w_out = np.asarray(w_out, f)[0]
    w_ff1 = np.asarray(w_ff1, f)[0]
    w_ff2 = np.asarray(w_ff2, f)[0]
    rel_bias = np.asarray(rel_bias, f)
    sinks = np.asarray(swa_sinks, f)[0]

    def kmaj(w):
        C = w.shape[1]
        return np.ascontiguousarray(w.reshape(16, 128, C).transpose(1, 0, 2).reshape(128, 16 * C))

    rep = lambda v: np.ascontiguousarray(np.broadcast_to(np.asarray(v, f)[None, :], (128, v.shape[0])))
    shared = {}
    shared["wmod"] = np.ascontiguousarray(
        w_mod.reshape(16, 128, 6, D).transpose(2, 0, 1, 3).reshape(6 * 16 * 128, D))
    shared["bmb"] = rep(b_mod)
    shared["gab"] = rep(np.asarray(attn_norm_g, f)[0])
    shared["gmb"] = rep(np.asarray(mlp_norm_g, f)[0])
    shared["gfb"] = rep(np.asarray(final_norm_g, f))
    q_cols = w_in[:, 0:1024]
    k0 = w_in[:, 1024:1088]
    k1 = w_in[:, 1088:1152]
    vv = w_in[:, 1152:1280]
    cq = w_in[:, 1280:1664]
    ckv = w_in[:, 1664:1792]
    kr = w_in[:, 1792:1856]
    kr_sw = np.concatenate([kr[:, 32:64], kr[:, 0:32]], axis=1)
    g2c = np.concatenate([k0, k0, k1, k1, vv, cq[:, 0:128]], axis=1)
    g3c = np.concatenate([cq[:, 128:384], np.zeros((D, 256), f)], axis=1)
    shared["winq"] = np.concatenate([kmaj(q_cols[:, 0:512]), kmaj(q_cols[:, 512:1024]), kmaj(g2c), kmaj(g3c)], axis=0)
    shared["wink"] = kmaj(np.concatenate([ckv, kr, kr, kr_sw, kr_sw], axis=1))
    uq = w_uq.reshape(384, 8, 192)
    nope = uq[:, :, 0:128].reshape(384, 1024)
    rope = uq[:, :, 128:192]
    rope_sw = np.concatenate([rope[:, :, 32:64], rope[:, :, 0:32]], axis=2)
    uqp = np.concatenate([nope, rope.reshape(384, 512), rope_sw.reshape(384, 512)], axis=1)
    uqp = uqp.reshape(3, 128, 2048).transpose(1, 0, 2).reshape(128, 6144)
    shared["wmla"] = np.ascontiguousarray(np.concatenate([uqp, w_ukv], axis=1))
    ukv = w_ukv.reshape(128, 8, 256)
    shared["wukT"] = np.ascontiguousarray(ukv[:, :, 0:128].transpose(2, 1, 0).reshape(128, 1024))
    shared["woutg"] = np.concatenate([kmaj(w_out[:, dg * 512:(dg + 1) * 512]) for dg in range(4)], axis=0)
    shared["w1g"] = np.concatenate([kmaj(w_ff1[:, g * 512:(g + 1) * 512]) for g in range(16)], axis=0)
    w2 = w_ff2.reshape(2, 4, 8, 128, 2, 1024)
    shared["w2g"] = np.ascontiguousarray(w2.transpose(0, 4, 1, 3, 2, 5).reshape(16 * 128, 8 * 1024))
    gqc = np.zeros((128, 4), f)
    gqc[:, 0:3] = np.asarray(mla_q_norm_g, f)[0].reshape(3, 128).T
    gqc[:, 3] = np.asarray(mla_kv_norm_g, f)[0]
    shared["gqc"] = gqc
    shared["sinkb"] = rep(sinks)
    ql = np.arange(128)[:, None]
    kl = np.arange(256)[None, :]
    dist = ql + 128 - kl
    inwin = (dist >= 0) & (dist < 128)
    tab = rel_bias[_t5_bucket(dist)]
    tab = np.where(inwin[:, :, None], tab, f(NEG)).astype(f)
    bm = tab.transpose(2, 0, 1).reshape(8, 2, 128, 256).transpose(0, 2, 1, 3).reshape(8 * 128, 512)
    shared["bmtab"] = np.ascontiguousarray(bm)
    dg_ = np.zeros((128, 4, 4, 128), f)
    for qb in range(4):
        for kb in range(4):
            if kb > qb:
                dg_[:, qb, kb, :] = NEG
            elif kb == qb:
                dg_[:, qb, kb, :] = np.where(np.arange(128)[None, :] <= np.arange(128)[:, None], 0.0, NEG)
    shared["diag"] = np.ascontiguousarray(dg_.reshape(128, 2048))
    half = 32
    inv_freq = (10000.0 ** (-np.arange(half, dtype=f) / half)).astype(f)
    pos = np.arange(S, dtype=f)
    ang = pos[None, :] * inv_freq[:, None]
    cosT = np.cos(ang).astype(f)
    sinT = np.sin(ang).astype(f)
    cos128 = np.concatenate([cosT] * 4, axis=0)
    sin128 = np.concatenate([-sinT, sinT, -sinT, sinT], axis=0)
    shared["cosk"] = np.ascontiguousarray(cos128)
    shared["sink"] = np.ascontiguousarray(sin128)
    shared["ident"] = np.eye(128, dtype=f)

    in_maps = []
    for core in cores:
        b, r = core // 2, core % 2
        m = dict(shared)
        m["xall"] = np.ascontiguousarray(x[b])
        xqv = np.zeros((4, 640, D), f)
        halom = np.zeros((4, 128, 128), f)
        msk = np.zeros((128, 16), f)
        cq_ = np.zeros((128, 4, 512), f)
        sq_ = np.zeros((128, 4, 512), f)
        for s_, t in enumerate(OWN[r]):
            if t == 0:
                xqv[s_, 128:] = x[b, 0:512]
                halom[s_] = NEG
            else:
                xqv[s_] = x[b, t * 512 - 128:t * 512 + 512]
            for which, kt in enumerate((TMIN[s_], TMAX[s_])):
                if kt < t:
                    a, bn = 0.0, 0.0
                elif kt == t:
                    a, bn = 1.0, 0.0
                else:
                    a, bn = 0.0, NEG
                msk[:, s_ * 4 + which * 2] = a
                msk[:, s_ * 4 + which * 2 + 1] = bn
            cq_[:, s_, :] = cos128[:, t * 512:(t + 1) * 512]
            sq_[:, s_, :] = sin128[:, t * 512:(t + 1) * 512]
        m["xq"] = xqv.reshape(4 * 640, D)
        m["halom"] = halom.reshape(4 * 128, 128)
        m["msk"] = msk
        m["cosq"] = cq_.reshape(128, 2048)
        m["sinq"] = sq_.reshape(128, 2048)
        m["ccol"] = np.ascontiguousarray(c[b].reshape(16, 128).T)
        in_maps.append(m)

    return in_maps


def kernel(**inputs):
    f = np.float32
    in_maps = _prep(**inputs)
    if "nc" not in _NC_CACHE:
        _NC_CACHE["nc"] = build_program()
    nc = _NC_CACHE["nc"]
    res = run_bass_kernel_spmd(nc, in_maps, core_ids=list(range(8)))
    outp = np.zeros((4, S, D), f)
    for core in range(8):
        b, r = core // 2, core % 2
        o = np.asarray(res.results[core]["out"], f).reshape(4, 512, D)
        for s_, t in enumerate(OWN[r]):
            outp[b, t * 512:(t + 1) * 512] = o[s_]
    return outp
```
